# Optimizing a Trainium2 kernel written in Bass

```python
import math
import jax, jax.numpy as jnp
from jax import lax
import numpy as np

D_MODEL = 2048
BATCH = 4
SEQ = 4096
DEPTH = 4

MIX_WIDTH = D_MODEL
HG_WIDTH = MIX_WIDTH // 2
DA_WIDTH = MIX_WIDTH - HG_WIDTH
HG_HEADS = 8
HG_DK = HG_WIDTH // HG_HEADS
HG_DV = HG_WIDTH // HG_HEADS
HG_CHUNK = 64
DA_HEADS = 8
DA_DV = DA_WIDTH // DA_HEADS
DA_DH = DA_DV // 2
Q_BLOCK = 128
D_FF = -(-8 * D_MODEL // (3 * 256)) * 256
EPS = 1e-6
LB_FLOOR = 1e-30
LB_CEIL = 1.0 - 1e-6
SPLIT_WIDTHS = [HG_WIDTH] * 5 + [DA_WIDTH] * 3
W_IN_COLS = sum(SPLIT_WIDTHS)

kernel_name = 'hybrid_hgrn2_diffattn_encoder'


def rmsnorm(x, w):
    xf = x.astype(jnp.float32)
    var = jnp.mean(xf * xf, axis=-1, keepdims=True)
    return (xf * lax.rsqrt(var + EPS)).astype(x.dtype) * w


def alibi_slopes(n):
    return jnp.asarray([2.0 ** (-8.0 * (i + 1) / n) for i in range(n)], dtype=jnp.float32)


def log_forget(z, lb):
    lb = jnp.clip(lb, 0.0, LB_CEIL)
    return jnp.logaddexp(jnp.log(jnp.maximum(lb, LB_FLOOR)),
                         jnp.log1p(-lb) + jax.nn.log_sigmoid(z))


def gla_chunk_scan(q, k, v, log_f):
    b, h, s, dk = q.shape
    dv = v.shape[-1]
    nc = s // HG_CHUNK

    def to_chunks(t):
        return jnp.moveaxis(t.reshape(b, h, nc, HG_CHUNK, t.shape[-1]), 2, 0)

    tri = jnp.tril(jnp.ones((HG_CHUNK, HG_CHUNK), dtype=bool))[:, :, None]

    def step(state, inp):
        qc, kc, vc, gc = inp
        a = jnp.cumsum(gc, axis=2)
        o_inter = jnp.einsum('bhtd,bhde->bhte', qc * jnp.exp(a), state)
        diff = a[:, :, :, None, :] - a[:, :, None, :, :]
        decay = jnp.where(tri, jnp.exp(jnp.minimum(diff, 0.0)), 0.0)
        scores = jnp.einsum('bhtd,bhsd,bhtsd->bhts', qc, kc, decay)
        o_intra = jnp.einsum('bhts,bhse->bhte', scores, vc)
        a_last = a[:, :, -1:, :]
        state = (jnp.exp(a_last)[:, :, 0, :, None] * state
                 + jnp.einsum('bhsd,bhse->bhde', kc * jnp.exp(a_last - a), vc))
        return state, o_inter + o_intra

    init = jnp.zeros((b, h, dk, dv), dtype=q.dtype)
    _, o = lax.scan(step, init, (to_chunks(q), to_chunks(k), to_chunks(v), to_chunks(log_f)))
    return jnp.moveaxis(o, 0, 2).reshape(b, h, s, dv)


def hgrn2_mixer(q, zf_fwd, zf_bwd, v, g, lb_fwd, lb_bwd, onorm_w):
    bsz, slen = q.shape[0], q.shape[1]

    def heads(t):
        return jnp.transpose(t.reshape(bsz, slen, HG_HEADS, -1).astype(jnp.float32), (0, 2, 1, 3))

    qh, vh = heads(q), heads(v)

    def one_direction(zf, lb, flip):
        lf = log_forget(heads(zf), lb.reshape(HG_HEADS, 1, HG_DK))
        kh = -jnp.expm1(lf)
        args = (qh, kh, vh, lf)
        if flip:
            args = tuple(jnp.flip(t, axis=2) for t in args)
        o = gla_chunk_scan(*args)
        return jnp.flip(o, axis=2) if flip else o

    o = one_direction(zf_fwd, lb_fwd, False) + one_direction(zf_bwd, lb_bwd, True)
    o = jnp.transpose(o, (0, 2, 1, 3))
    o = rmsnorm(o, onorm_w) * jax.nn.silu(g.reshape(o.shape).astype(jnp.float32))
    return o.reshape(bsz, slen, HG_WIDTH)


def diff_attention(q, k, v, qn_w, kn_w, lam, lam_init, subln_w):
    bsz, slen = q.shape[0], q.shape[1]
    q = rmsnorm(q.reshape(bsz, slen, DA_HEADS, 2, DA_DH), qn_w) * (DA_DH ** -0.5)
    k = rmsnorm(k.reshape(bsz, slen, DA_HEADS, 2, DA_DH), kn_w)
    v = v.reshape(bsz, slen, DA_HEADS, DA_DV)
    slopes = alibi_slopes(DA_HEADS)
    nb = slen // Q_BLOCK
    q_blocks = jnp.moveaxis(q.reshape(bsz, nb, Q_BLOCK, DA_HEADS, 2, DA_DH), 1, 0)
    starts = jnp.arange(nb, dtype=jnp.int32) * Q_BLOCK
    pos_k = jnp.arange(slen, dtype=jnp.int32)

    def block(args):
        q_blk, start = args
        s = jnp.einsum('bthmd,bshmd->bhmts', q_blk, k).astype(jnp.float32)
        pos_q = start + jnp.arange(Q_BLOCK, dtype=jnp.int32)
        dist = jnp.abs(pos_q[:, None] - pos_k[None, :]).astype(jnp.float32)
        s = s - slopes[None, :, None, None, None] * dist
        p = jax.nn.softmax(s, axis=-1)
        attn = p[:, :, 0] - lam * p[:, :, 1]
        return jnp.einsum('bhts,bshe->bthe', attn.astype(v.dtype), v)

    o = lax.map(block, (q_blocks, starts))
    o = jnp.moveaxis(o, 0, 1).reshape(bsz, slen, DA_HEADS, DA_DV)
    o = rmsnorm(o, subln_w) * (1.0 - lam_init)
    return o.reshape(bsz, slen, DA_WIDTH)


def setup_inputs(seed: int = 0) -> dict:
    key = jax.random.key(seed)
    ks = jax.random.split(key, 16)
    f32 = jnp.float32

    def nrm(k, shape, scale):
        return jax.random.normal(k, shape, dtype=f32) * scale

    return {
        'x': nrm(ks[0], (BATCH, SEQ, D_MODEL), 1.0),
        'norm_mix_w': 1.0 + nrm(ks[1], (DEPTH, D_MODEL), 0.02),
        'w_in': nrm(ks[2], (DEPTH, D_MODEL, W_IN_COLS), D_MODEL ** -0.5),
        'hg_lb_logits': 1.0 + nrm(ks[3], (2, DEPTH, HG_WIDTH), 0.1),
        'hg_onorm_w': 1.0 + nrm(ks[4], (DEPTH, HG_DV), 0.02),
        'da_qnorm_w': 1.0 + nrm(ks[5], (DEPTH, DA_DH), 0.02),
        'da_knorm_w': 1.0 + nrm(ks[6], (DEPTH, DA_DH), 0.02),
        'da_lambda': nrm(ks[7], (DEPTH, 4, DA_DH), 0.1),
        'da_subln_w': 1.0 + nrm(ks[8], (DEPTH, DA_DV), 0.02),
        'w_out': nrm(ks[9], (DEPTH, MIX_WIDTH, D_MODEL), MIX_WIDTH ** -0.5),
        'norm_ffn_w': 1.0 + nrm(ks[10], (DEPTH, D_MODEL), 0.02),
        'w_gate': nrm(ks[11], (DEPTH, D_MODEL, D_FF), D_MODEL ** -0.5),
        'w_up': nrm(ks[12], (DEPTH, D_MODEL, D_FF), D_MODEL ** -0.5),
        'w_down': nrm(ks[13], (DEPTH, D_FF, D_MODEL), D_FF ** -0.5),
    }


def reference(x, norm_mix_w, w_in, hg_lb_logits, hg_onorm_w, da_qnorm_w, da_knorm_w,
              da_lambda, da_subln_w, w_out, norm_ffn_w, w_gate, w_up, w_down):
    p_lb = jax.nn.softmax(hg_lb_logits.astype(jnp.float32), axis=1)
    lower_bounds = jnp.cumsum(p_lb, axis=1) - p_lb[:, :1]
    split_idx = list(np.cumsum(SPLIT_WIDTHS)[:-1].tolist())

    for l in range(DEPTH):
        h = rmsnorm(x, norm_mix_w[l])
        proj = jnp.einsum('bsd,dc->bsc', h, w_in[l])
        hq, hf_f, hf_b, hi, hg, dq, dk, dv = jnp.split(proj, split_idx, axis=-1)

        o_hg = hgrn2_mixer(hq, hf_f, hf_b, hi, hg,
                           lower_bounds[0, l], lower_bounds[1, l], hg_onorm_w[l])

        lam_p = da_lambda[l].astype(jnp.float32)
        lam_init = 0.8 - 0.6 * math.exp(-0.3 * l)
        lam = (jnp.exp(jnp.sum(lam_p[0] * lam_p[1])) - jnp.exp(jnp.sum(lam_p[2] * lam_p[3]))
               + lam_init)
        o_da = diff_attention(dq, dk, dv, da_qnorm_w[l], da_knorm_w[l], lam, lam_init,
                              da_subln_w[l])

        mixed = jnp.concatenate([o_hg.astype(x.dtype), o_da.astype(x.dtype)], axis=-1)
        x = x + jnp.einsum('bsc,cd->bsd', mixed, w_out[l])

        h = rmsnorm(x, norm_ffn_w[l])
        u = jax.nn.silu(jnp.einsum('bsd,df->bsf', h, w_gate[l])) * jnp.einsum('bsd,df->bsf', h, w_up[l])
        x = x + jnp.einsum('bsf,fd->bsd', u, w_down[l])
    return x
```

```python
import math
import numpy as np
import concourse.bass as bass
import concourse.mybir as mybir
from concourse.bass_utils import run_bass_kernel_spmd
import ml_dtypes

F32 = mybir.dt.float32
BF16 = mybir.dt.bfloat16
AF = mybir.ActivationFunctionType
ALU = mybir.AluOpType
AX = mybir.AxisListType
NPBF16 = ml_dtypes.bfloat16

D = 2048
S = 4096
NBATCH = 4
DEPTH = 4
DFF = 5632
NFC = DFF // 128
EPS = 1e-6
TOKC = 2048
NSTC = TOKC // 512
NST = S // 512
NCH = S // 64
ENGS = ['pe', 'act', 'dve', 'pool', 'sp']


class Buf:
    __slots__ = ('name', 'w', 'r')

    def __init__(self, name):
        self.name = name
        self.w = {}
        self.r = {}


class Prog:
    SELF_SYNC = True

    def __init__(self):
        self.nc = bass.Bass("TRN2", target_bir_lowering=False)
        self.ops = {e: [] for e in ENGS}
        self.ecnt = {e: 0 for e in ENGS}
        self.waited = {e: {} for e in ENGS}
        self.sems = {}
        self.dcnt = {}
        for e in ENGS:
            self.sem(('e', e))

    def sem(self, key):
        if key not in self.sems:
            self.sems[key] = self.nc.alloc_semaphore(name="s_%s_%s" % key)
            if key[0] == 'd':
                self.dcnt[key] = 0
        return self.sems[key]

    def _wait(self, E, k, v):
        wd = self.waited[E]
        if wd.get(k, 0) >= v:
            return
        wd[k] = v
        s = self.sem(k)
        self.ops[E].append(lambda eng, s=s, v=v: eng.wait_ge(s, v))

    def _deps(self, E, reads, writes, skip=None):
        need = {}
        own = ('e', E)
        raw_own = 0
        for b in reads:
            for k, v in b.w.items():
                if need.get(k, 0) < v:
                    need[k] = v
                if k == own and v > raw_own:
                    raw_own = v
        for b in writes:
            for k, v in b.w.items():
                if need.get(k, 0) < v:
                    need[k] = v
            for k, v in b.r.items():
                if need.get(k, 0) < v:
                    need[k] = v
        for k, v in need.items():
            if k == own or k == skip:
                continue
            self._wait(E, k, v)
        if self.SELF_SYNC and raw_own > 0 and E in ('act', 'dve', 'pool'):
            self._wait(E, own, raw_own)

    def op(self, E, fn, reads=(), writes=(), writes_nw=()):
        self._deps(E, reads, writes)
        self.ecnt[E] += 1
        seq = self.ecnt[E]
        k = ('e', E)
        s = self.sems[k]
        self.ops[E].append(lambda eng, fn=fn, s=s: fn(eng).then_inc(s, 1))
        for b in writes:
            b.w[k] = seq
        for b in writes_nw:
            b.w[k] = seq
        for b in reads:
            b.r[k] = seq
        return seq

    def dma(self, Q, out, in_, dsem, reads=(), writes=(), writes_nw=(), cont=False, **kw):
        k = ('d', dsem)
        self._deps(Q, reads, writes, skip=(k if cont else None))
        s = self.sem(k)
        self.dcnt[k] += 16
        c = self.dcnt[k]
        self.ops[Q].append(
            lambda eng, out=out, in_=in_, s=s, kw=kw: eng.dma_start(out=out, in_=in_, **kw).then_inc(s, 16))
        for b in writes:
            b.w[k] = c
        for b in writes_nw:
            b.w[k] = c
        for b in reads:
            b.r[k] = c

    def barrier(self):
        for E in ENGS:
            for e in ENGS:
                if e != E and self.ecnt[e] > 0:
                    self._wait(E, ('e', e), self.ecnt[e])
            for k, c in self.dcnt.items():
                if c > 0:
                    self._wait(E, k, c)

    def finish(self):
        self.barrier()

    def emit(self):
        nc = self.nc
        ops = self.ops
        with nc.Block() as block:
            @block.tensor
            def _(eng):
                for f in ops['pe']:
                    f(eng)

            @block.scalar
            def _(eng):
                for f in ops['act']:
                    f(eng)

            @block.vector
            def _(eng):
                for f in ops['dve']:
                    f(eng)

            @block.gpsimd
            def _(eng):
                for f in ops['pool']:
                    f(eng)

            @block.sync
            def _(eng):
                for f in ops['sp']:
                    f(eng)
        return nc


class T:
    __slots__ = ('ap', 'buf')

    def __init__(self, ap, buf):
        self.ap = ap
        self.buf = buf


class Ctx:
    ARENA_WORDS = 48 * 1024

    def __init__(self):
        self.p = Prog()
        self.nc = self.p.nc
        self.arena = self.nc.alloc_sbuf_tensor("arena", [128, self.ARENA_WORDS], F32).ap()
        self.off = 0
        self.banks = []
        self.bbuf = []
        for i in range(8):
            self.banks.append(self.nc.alloc_psum_tensor("bank%d" % i, [128, 512], F32).ap())
            self.bbuf.append(Buf("bank%d" % i))
        self.uid = 0
        self.dram = {}

    def _view(self, a, dims):
        if len(dims) == 1:
            return a
        names = ["d%d" % i for i in range(len(dims))]
        pat = "p (%s) -> p %s" % (" ".join(names), " ".join(names))
        return a.rearrange(pat, **{n: d for n, d in zip(names, dims)})

    def f32(self, name, *dims, parts=128):
        n = int(np.prod(dims))
        a = self.arena[0:parts, self.off:self.off + n]
        self.off += n
        assert self.off <= self.ARENA_WORDS, "SBUF arena overflow at %s: %d" % (name, self.off)
        self.uid += 1
        return T(self._view(a, dims), Buf("%s_%d" % (name, self.uid)))

    def bf16(self, name, *dims, parts=128):
        n = int(np.prod(dims))
        assert n % 2 == 0
        a = self.arena[0:parts, self.off:self.off + n // 2].bitcast(BF16)
        self.off += n // 2
        assert self.off <= self.ARENA_WORDS, "SBUF arena overflow at %s: %d" % (name, self.off)
        self.uid += 1
        return T(self._view(a, dims), Buf("%s_%d" % (name, self.uid)))

    def stage_begin(self, mark):
        self.p.barrier()
        self.off = mark

    def dram_t(self, name, shape, dtype, kind="Internal"):
        h = self.nc.dram_tensor(name, list(shape), dtype, kind=kind)
        self.dram[name] = h
        return h.ap()


def bank_bf(ctx, i):
    return ctx.banks[i].bitcast(BF16)


def load_consts(ctx, cin):
    p = ctx.p
    c = {}
    c['ident'] = ctx.bf16("ident", 128)
    c['ones_bf'] = ctx.bf16("ones_bf", 128)
    c['ones_f'] = ctx.f32("ones_f", 128)
    p.dma('sp', c['ident'].ap, cin['c_ident'], 'c_ident', writes=[c['ident'].buf])
    p.dma('sp', c['ones_bf'].ap, cin['c_ones_bf'], 'c_ones_bf', writes=[c['ones_bf'].buf])
    p.dma('sp', c['ones_f'].ap, cin['c_ones_f'], 'c_ones_f', writes=[c['ones_f'].buf])
    return c


def rstd_small(ctx, src_ap, src_buf, dst, scale, tmp):
    p = ctx.p
    p.op('act', lambda e: e.activation(out=tmp.ap, in_=src_ap, func=AF.Ln, bias=EPS, scale=scale),
         reads=[src_buf], writes=[tmp.buf])
    p.op('act', lambda e: e.activation(out=dst.ap, in_=tmp.ap, func=AF.Exp, scale=-0.5),
         reads=[tmp.buf], writes=[dst.buf])


class NormTr:
    def __init__(self, ctx, consts, bank_pairs):
        self.ctx = ctx
        self.c = consts
        self.junk = ctx.bf16("nt_junk", D)
        self.hb = [ctx.bf16("nt_hb", D), ctx.bf16("nt_hb", D)]
        self.ss = [ctx.f32("nt_ss", 2), ctx.f32("nt_ss", 2)]
        self.t1 = [ctx.f32("nt_t1", 2), ctx.f32("nt_t1", 2)]
        self.rs = [ctx.f32("nt_rs", 2), ctx.f32("nt_rs", 2)]
        self.bank_pairs = bank_pairs
        self.n = 0

    def run(self, xs, wb, hst, k):
        ctx, p = self.ctx, self.ctx.p
        i = self.n % 2
        bp = self.bank_pairs[self.n % len(self.bank_pairs)]
        self.n += 1
        junk, hb, ss, t1, rs = self.junk, self.hb[i], self.ss[i], self.t1[i], self.rs[i]
        p.op('act', lambda e: e.activation(out=junk.ap, in_=xs.ap, func=AF.Square, accum_out=ss.ap[:, 0:1]),
             reads=[xs.buf], writes=[junk.buf, ss.buf])
        p.op('act', lambda e: e.activation(out=t1.ap[:, 0:1], in_=ss.ap[:, 0:1], func=AF.Ln, bias=EPS, scale=1.0 / D),
             reads=[ss.buf], writes=[t1.buf])
        p.op('act', lambda e: e.activation(out=rs.ap[:, 0:1], in_=t1.ap[:, 0:1], func=AF.Exp, scale=-0.5),
             reads=[t1.buf], writes=[rs.buf])
        p.op('dve', lambda e: e.scalar_tensor_tensor(out=hb.ap, in0=xs.ap, scalar=rs.ap[:, 0:1], in1=wb.ap,
                                                     op0=ALU.mult, op1=ALU.mult),
             reads=[xs.buf, rs.buf, wb.buf], writes=[hb.buf])
        ident = self.c['ident']
        for h2 in range(2):
            bi = bp[h2]
            bv = bank_bf(ctx, bi)
            for c8 in range(8):
                cc = h2 * 8 + c8
                p.op('pe', lambda e, bv=bv, c8=c8, cc=cc: e.transpose(bv[:, c8 * 128:(c8 + 1) * 128],
                                                                   hb.ap[:, cc * 128:(cc + 1) * 128], ident.ap),
                     reads=[hb.buf, ident.buf], writes=[ctx.bbuf[bi]])
            src = bv.rearrange("p (a b) -> p a b", a=8)
            dst = hst.ap[:, h2 * 8:(h2 + 1) * 8, k * 128:(k + 1) * 128]
            if h2 == 0:
                p.op('act', lambda e, src=src, dst=dst: e.copy(out=dst, in_=src),
                     reads=[ctx.bbuf[bi]], writes=[hst.buf])
            else:
                p.op('dve', lambda e, src=src, dst=dst: e.tensor_copy(out=dst, in_=src),
                     reads=[ctx.bbuf[bi]], writes=[hst.buf])


def stage_p1(ctx, consts, mark, x_d, nw_row_d, hT_d):
    p = ctx.p
    ctx.stage_begin(mark)
    wb = ctx.f32("p1_wb", D)
    p.dma('sp', wb.ap, nw_row_d.partition_broadcast(128), 'p1_wb', writes=[wb.buf])
    xs = [ctx.f32("p1_xs", D), ctx.f32("p1_xs", D)]
    hst = [ctx.bf16("p1_hst", 16, 512), ctx.bf16("p1_hst", 16, 512)]
    nt = NormTr(ctx, consts, [(0, 1), (2, 3)])
    hT_buf = Buf("hT_d")
    ntile = TOKC // 128
    p.dma('sp', xs[0].ap, x_d[0:128, :], 'p1_xs0', writes=[xs[0].buf])
    for t in range(ntile):
        st, k = t // 4, t % 4
        if t + 1 < ntile:
            j = (t + 1) % 2
            p.dma('sp', xs[j].ap, x_d[(t + 1) * 128:(t + 2) * 128, :], 'p1_xs%d' % j, writes=[xs[j].buf])
        nt.run(xs[t % 2], wb, hst[st % 2], k)
        if k == 3:
            p.dma('sp', hT_d[st], hst[st % 2].ap, 'p1_hst%d' % (st % 2), reads=[hst[st % 2].buf], writes_nw=[hT_buf])
    return hT_buf


def stage_p3a(ctx, consts, mark, x_d, mt_d, wout_d, nfw_row_d, x1_d, h2T_d):
    p = ctx.p
    ctx.stage_begin(mark)
    wo = ctx.bf16("p3a_wo", 16, D)
    wb = ctx.f32("p3a_wb", D)
    wsrc = wout_d.rearrange("(c p) n -> p c n", p=128)
    for q in range(4):
        p.dma('pool', wo.ap[:, :, q * 512:(q + 1) * 512], wsrc[:, :, q * 512:(q + 1) * 512], 'p3a_wo', writes=[wo.buf],
              cont=(q > 0))
    p.dma('sp', wb.ap, nfw_row_d.partition_broadcast(128), 'p3a_wb', writes=[wb.buf])
    xs = [ctx.f32("p3a_xs", D), ctx.f32("p3a_xs", D)]
    mt = [ctx.bf16("p3a_mt", 16, 512), ctx.bf16("p3a_mt", 16, 512)]
    hst = [ctx.bf16("p3a_hst", 16, 512), ctx.bf16("p3a_hst", 16, 512)]
    nt = NormTr(ctx, consts, [(4, 5), (6, 7)])
    x1_buf, h2T_buf = Buf("x1_d"), Buf("h2T_d")

    def load_mt(st):
        j = st % 2
        for r in range(2):
            p.dma('sp', mt[j].ap[:, r * 8:(r + 1) * 8, :], mt_d[r, :, :, st * 512:(st + 1) * 512], 'p3a_mt%d' % j,
                  writes=[mt[j].buf], cont=(r > 0))

    def load_x(t):
        j = t % 2
        p.dma('sp', xs[j].ap, x_d[t * 128:(t + 1) * 128, :], 'p3a_xs%d' % j, writes=[xs[j].buf])

    load_mt(0)
    load_x(0)
    ntile = TOKC // 128
    for t in range(ntile):
        st, k = t // 4, t % 4
        if k == 0 and st + 1 < NSTC:
            load_mt(st + 1)
        if t + 1 < ntile:
            load_x(t + 1)
        x_t = xs[t % 2]
        m_t = mt[st % 2]
        for dt in range(4):
            for c in range(16):
                p.op('pe', lambda e, dt=dt, c=c, m_t=m_t, k=k: e.matmul(
                    ctx.banks[dt], lhsT=m_t.ap[:, c, k * 128:(k + 1) * 128], rhs=wo.ap[:, c, dt * 512:(dt + 1) * 512],
                    start=(c == 0), stop=(c == 15)),
                    reads=[m_t.buf, wo.buf], writes=[ctx.bbuf[dt]])
            p.op('dve', lambda e, dt=dt, x_t=x_t: e.tensor_tensor(
                out=x_t.ap[:, dt * 512:(dt + 1) * 512], in0=ctx.banks[dt], in1=x_t.ap[:, dt * 512:(dt + 1) * 512],
                op=ALU.add), reads=[ctx.bbuf[dt], x_t.buf], writes=[x_t.buf])
        p.dma('sp', x1_d[t * 128:(t + 1) * 128, :], x_t.ap, 'p3a_xo%d' % (t % 2), reads=[x_t.buf], writes_nw=[x1_buf])
        nt.run(x_t, wb, hst[st % 2], k)
        if k == 3:
            p.dma('sp', h2T_d[st], hst[st % 2].ap, 'p3a_hst%d' % (st % 2), reads=[hst[st % 2].buf], writes_nw=[h2T_buf])
    return x1_buf, h2T_buf


def stage_p3b(ctx, consts, mark, h2T_d, h2T_buf, wg_d, wu_d, uT_d):
    p = ctx.p
    ctx.stage_begin(mark)
    hT = [ctx.bf16("p3b_hT", 16, 512) for _ in range(NSTC)]
    for st in range(NSTC):
        p.dma('sp', hT[st].ap, h2T_d[st], 'p3b_hT%d' % st, reads=[h2T_buf], writes=[hT[st].buf])
    wg = [ctx.bf16("p3b_wg", 16, 512), ctx.bf16("p3b_wg", 16, 512)]
    wu = [ctx.bf16("p3b_wu", 16, 512), ctx.bf16("p3b_wu", 16, 512)]
    sg = [ctx.f32("p3b_sg", 512), ctx.f32("p3b_sg", 512)]
    ub = [ctx.bf16("p3b_ub", 4, 512), ctx.bf16("p3b_ub", 4, 512)]
    uT_buf = Buf("uT_d")
    ngrp = DFF // 512
    wgs = wg_d.rearrange("(c p) n -> p c n", p=128)
    wus = wu_d.rearrange("(c p) n -> p c n", p=128)

    def load_w(g):
        j = g % 2
        p.dma('pool', wg[j].ap, wgs[:, :, g * 512:(g + 1) * 512], 'p3b_wg%d' % j, writes=[wg[j].buf])
        p.dma('pool', wu[j].ap, wus[:, :, g * 512:(g + 1) * 512], 'p3b_wu%d' % j, writes=[wu[j].buf])

    load_w(0)
    cnt = 0
    for g in range(ngrp):
        if g + 1 < ngrp:
            load_w(g + 1)
        wg_t, wu_t = wg[g % 2], wu[g % 2]
        for st in range(NSTC):
            ub_t = ub[(g * NSTC + st) % 2]
            for fc in range(4):
                bg, bu = 2 * (cnt % 4), 2 * (cnt % 4) + 1
                sg_t = sg[cnt % 2]
                cnt += 1
                for c in range(16):
                    p.op('pe', lambda e, c=c, fc=fc, st=st, bg=bg, wg_t=wg_t: e.matmul(
                        ctx.banks[bg], lhsT=wg_t.ap[:, c, fc * 128:(fc + 1) * 128], rhs=hT[st].ap[:, c, :],
                        start=(c == 0), stop=(c == 15)), reads=[wg_t.buf, hT[st].buf], writes=[ctx.bbuf[bg]])
                for c in range(16):
                    p.op('pe', lambda e, c=c, fc=fc, st=st, bu=bu, wu_t=wu_t: e.matmul(
                        ctx.banks[bu], lhsT=wu_t.ap[:, c, fc * 128:(fc + 1) * 128], rhs=hT[st].ap[:, c, :],
                        start=(c == 0), stop=(c == 15)), reads=[wu_t.buf, hT[st].buf], writes=[ctx.bbuf[bu]])
                p.op('act', lambda e, bg=bg, sg_t=sg_t: e.activation(out=sg_t.ap, in_=ctx.banks[bg], func=AF.Silu),
                     reads=[ctx.bbuf[bg]], writes=[sg_t.buf])
                p.op('dve', lambda e, bu=bu, sg_t=sg_t, ub_t=ub_t, fc=fc: e.tensor_tensor(
                    out=ub_t.ap[:, fc, :], in0=ctx.banks[bu], in1=sg_t.ap, op=ALU.mult),
                    reads=[ctx.bbuf[bu], sg_t.buf], writes=[ub_t.buf])
            p.dma('sp', uT_d[st, :, g * 4:(g + 1) * 4, :], ub_t.ap, 'p3b_ub%d' % ((g * NSTC + st) % 2),
                  reads=[ub_t.buf], writes_nw=[uT_buf])
    return uT_buf


def stage_p3c(ctx, consts, mark, uT_d, uT_buf, x1_d, x1_buf, wd_d, x2_d):
    p = ctx.p
    ctx.stage_begin(mark)
    wd = [ctx.bf16("p3c_wd", NFC, 512), ctx.bf16("p3c_wd", NFC, 512)]
    ut = [ctx.bf16("p3c_ut", NFC, 512), ctx.bf16("p3c_ut", NFC, 512)]
    xin = [ctx.f32("p3c_xin", 512), ctx.f32("p3c_xin", 512)]
    xo = [ctx.f32("p3c_xo", 512), ctx.f32("p3c_xo", 512)]
    x2_buf = Buf("x2_d")
    wds = wd_d.rearrange("(c p) n -> p c n", p=128)

    def load_wd(dt):
        j = dt % 2
        for h in range(2):
            p.dma('pool', wd[j].ap[:, h * 22:(h + 1) * 22, :], wds[:, h * 22:(h + 1) * 22, dt * 512:(dt + 1) * 512],
                  'p3c_wd%d' % j, writes=[wd[j].buf], cont=(h > 0))

    def load_ut(i):
        j = i % 2
        st = i % NSTC
        p.dma('sp', ut[j].ap, uT_d[st], 'p3c_ut%d' % j, reads=[uT_buf], writes=[ut[j].buf])

    load_wd(0)
    load_ut(0)
    it = 0
    cnt = 0
    for dt in range(4):
        if dt + 1 < 4:
            load_wd(dt + 1)
        wd_t = wd[dt % 2]
        for st in range(NSTC):
            if it + 1 < 4 * NSTC:
                load_ut(it + 1)
            ut_t = ut[it % 2]
            it += 1
            for k in range(4):
                b = cnt % 8
                xi, xo_t = xin[cnt % 2], xo[cnt % 2]
                r0 = st * 512 + k * 128
                p.dma('sp', xi.ap, x1_d[r0:r0 + 128, dt * 512:(dt + 1) * 512], 'p3c_xin%d' % (cnt % 2),
                      reads=[x1_buf], writes=[xi.buf])
                for fc in range(NFC):
                    p.op('pe', lambda e, fc=fc, k=k, b=b, ut_t=ut_t, wd_t=wd_t: e.matmul(
                        ctx.banks[b], lhsT=ut_t.ap[:, fc, k * 128:(k + 1) * 128], rhs=wd_t.ap[:, fc, :],
                        start=(fc == 0), stop=(fc == NFC - 1)), reads=[ut_t.buf, wd_t.buf], writes=[ctx.bbuf[b]])
                p.op('dve', lambda e, b=b, xi=xi, xo_t=xo_t: e.tensor_tensor(out=xo_t.ap, in0=ctx.banks[b], in1=xi.ap,
                                                                         op=ALU.add),
                     reads=[ctx.bbuf[b], xi.buf], writes=[xo_t.buf])
                p.dma('sp', x2_d[r0:r0 + 128, dt * 512:(dt + 1) * 512], xo_t.ap, 'p3c_xo%d' % (cnt % 2),
                      reads=[xo_t.buf], writes_nw=[x2_buf])
                cnt += 1
    return x2_buf


def host_consts():
    return {
        'c_ident': np.eye(128, dtype=np.float32).astype(NPBF16),
        'c_ones_bf': np.ones((128, 128), dtype=NPBF16),
        'c_ones_f': np.ones((128, 128), dtype=np.float32),
    }


def declare_consts(ctx):
    cin = {}
    cin['c_ident'] = ctx.dram_t('c_ident', [128, 128], BF16, "ExternalInput")
    cin['c_ones_bf'] = ctx.dram_t('c_ones_bf', [128, 128], BF16, "ExternalInput")
    cin['c_ones_f'] = ctx.dram_t('c_ones_f', [128, 128], F32, "ExternalInput")
    return cin


def build_A():
    ctx = Ctx()
    cin = declare_consts(ctx)
    x = ctx.dram_t("x", [TOKC, D], F32, "ExternalInput")
    nw = ctx.dram_t("nmw", [1, D], F32, "ExternalInput")
    hT = ctx.dram_t("hT", [NSTC, 128, 16, 512], BF16, "ExternalOutput")
    consts = load_consts(ctx, cin)
    mark = ctx.off
    stage_p1(ctx, consts, mark, x, nw, hT)
    ctx.p.finish()
    return ctx.p.emit()


def build_C(with_p1=True, dbg=False):
    ctx = Ctx()
    IK = "ExternalOutput" if dbg else "Internal"
    cin = declare_consts(ctx)
    x = ctx.dram_t("x", [TOKC, D], F32, "ExternalInput")
    mt = ctx.dram_t("mt", [2, 128, 8, TOKC], BF16, "ExternalInput")
    wout = ctx.dram_t("wout", [D, D], F32, "ExternalInput")
    nfw = ctx.dram_t("nfw", [1, D], F32, "ExternalInput")
    wg = ctx.dram_t("wg", [D, DFF], F32, "ExternalInput")
    wu = ctx.dram_t("wu", [D, DFF], F32, "ExternalInput")
    wd = ctx.dram_t("wd", [DFF, D], F32, "ExternalInput")
    nmw = ctx.dram_t("nmw", [1, D], F32, "ExternalInput")
    x1 = ctx.dram_t("x1", [TOKC, D], F32, IK)
    h2T = ctx.dram_t("h2T", [NSTC, 128, 16, 512], BF16, IK)
    uT = ctx.dram_t("uT", [NSTC, 128, NFC, 512], BF16, IK)
    x2 = ctx.dram_t("x2", [TOKC, D], F32, "ExternalOutput")
    hT = ctx.dram_t("hT", [NSTC, 128, 16, 512], BF16, "ExternalOutput")
    consts = load_consts(ctx, cin)
    mark = ctx.off
    x1_buf, h2T_buf = stage_p3a(ctx, consts, mark, x, mt, wout, nfw, x1, h2T)
    uT_buf = stage_p3b(ctx, consts, mark, h2T, h2T_buf, wg, wu, uT)
    stage_p3c(ctx, consts, mark, uT, uT_buf, x1, x1_buf, wd, x2)
    if with_p1:
        stage_p1(ctx, consts, mark, x2, nmw, hT)
    ctx.p.finish()
    return ctx.p.emit()


P2A_KIND = ['F', 'F', 'F', 'T', 'F', 'Q', 'K', 'T']


def stage_p2a(ctx, consts, mark, hT_d, hT_buf, win_d, qnw_row_d, knw_row_d, outs):
    p = ctx.p
    ctx.stage_begin(mark)
    hT = [ctx.bf16("p2a_hT", 16, 512) for _ in range(4)]
    w = [ctx.bf16("p2a_w", 16, 512), ctx.bf16("p2a_w", 16, 512)]
    fst = [ctx.f32("p2a_fst", 512), ctx.f32("p2a_fst", 512)]
    tst = [ctx.bf16("p2a_tst", 512), ctx.bf16("p2a_tst", 512)]
    sq = [ctx.f32("p2a_sq", 512), ctx.f32("p2a_sq", 512)]
    tq = [ctx.f32("p2a_tq", 512), ctx.f32("p2a_tq", 512)]
    ssq = [ctx.f32("p2a_ssq", 8), ctx.f32("p2a_ssq", 8)]
    t1 = [ctx.f32("p2a_t1", 8), ctx.f32("p2a_t1", 8)]
    rs = [ctx.f32("p2a_rs", 8), ctx.f32("p2a_rs", 8)]
    qw = ctx.f32("p2a_qw", 64)
    kw = ctx.f32("p2a_kw", 64)
    p.dma('sp', qw.ap, qnw_row_d.partition_broadcast(128), 'p2a_qw', writes=[qw.buf])
    p.dma('sp', kw.ap, knw_row_d.partition_broadcast(128), 'p2a_kw', writes=[kw.buf])
    p.op('dve', lambda e: e.tensor_scalar(out=qw.ap, in0=qw.ap, scalar1=0.125, scalar2=None, op0=ALU.mult),
         reads=[qw.buf], writes=[qw.buf])
    obuf = {g: Buf("p2a_out%d" % g) for g in range(8)}
    wsrc = win_d.rearrange("(c p) n -> p c n", p=128)
    wcnt = 0
    fcnt = 0
    tcnt = 0
    bcnt = 0

    def load_w(g, slot):
        p.dma('pool', w[slot].ap, wsrc[:, :, g * 512:(g + 1) * 512], 'p2a_w%d' % slot, writes=[w[slot].buf])

    for half in range(2):
        for sl in range(4):
            p.dma('sp', hT[sl].ap, hT_d[half * 4 + sl], 'p2a_hT%d' % sl, reads=[hT_buf], writes=[hT[sl].buf])
        load_w(0, wcnt % 2)
        for g in range(8):
            w_t = w[wcnt % 2]
            wcnt += 1
            if g + 1 < 8:
                load_w(g + 1, wcnt % 2)
            kind = P2A_KIND[g]
            for sl in range(4):
                st = half * 4 + sl
                for u in range(4):
                    b = bcnt % 8
                    bcnt += 1
                    if kind == 'F':
                        for c in range(16):
                            p.op('pe', lambda e, c=c, u=u, b=b, w_t=w_t, sl=sl: e.matmul(
                                ctx.banks[b], lhsT=w_t.ap[:, c, u * 128:(u + 1) * 128], rhs=hT[sl].ap[:, c, :],
                                start=(c == 0), stop=(c == 15)), reads=[w_t.buf, hT[sl].buf], writes=[ctx.bbuf[b]])
                        f_t = fst[fcnt % 2]
                        if fcnt % 2 == 0:
                            p.op('act', lambda e, b=b, f_t=f_t: e.copy(out=f_t.ap, in_=ctx.banks[b]),
                                 reads=[ctx.bbuf[b]], writes=[f_t.buf])
                        else:
                            p.op('dve', lambda e, b=b, f_t=f_t: e.tensor_copy(out=f_t.ap, in_=ctx.banks[b]),
                                 reads=[ctx.bbuf[b]], writes=[f_t.buf])
                        p.dma('sp', outs[g][u, :, st * 512:(st + 1) * 512], f_t.ap, 'p2a_fst%d' % (fcnt % 2),
                              reads=[f_t.buf], writes_nw=[obuf[g]])
                        fcnt += 1
                    else:
                        for c in range(16):
                            p.op('pe', lambda e, c=c, u=u, b=b, w_t=w_t, sl=sl: e.matmul(
                                ctx.banks[b], lhsT=hT[sl].ap[:, c, u * 128:(u + 1) * 128], rhs=w_t.ap[:, c, :],
                                start=(c == 0), stop=(c == 15)), reads=[w_t.buf, hT[sl].buf], writes=[ctx.bbuf[b]])
                        i = tcnt % 2
                        t_t = tst[i]
                        if kind == 'T':
                            if tcnt % 2 == 0:
                                p.op('act', lambda e, b=b, t_t=t_t: e.copy(out=t_t.ap, in_=ctx.banks[b]),
                                     reads=[ctx.bbuf[b]], writes=[t_t.buf])
                            else:
                                p.op('dve', lambda e, b=b, t_t=t_t: e.tensor_copy(out=t_t.ap, in_=ctx.banks[b]),
                                     reads=[ctx.bbuf[b]], writes=[t_t.buf])
                        else:
                            nw = qw if kind == 'Q' else kw
                            sq_t, tq_t, ssq_t, t1_t, rs_t = sq[i], tq[i], ssq[i], t1[i], rs[i]
                            p.op('act', lambda e, b=b, sq_t=sq_t: e.activation(out=sq_t.ap, in_=ctx.banks[b], func=AF.Square),
                                 reads=[ctx.bbuf[b]], writes=[sq_t.buf])
                            p.op('dve', lambda e, sq_t=sq_t, ssq_t=ssq_t: e.tensor_reduce(
                                out=ssq_t.ap, in_=sq_t.ap.rearrange("p (a b) -> p a b", a=8), axis=AX.X, op=ALU.add),
                                reads=[sq_t.buf], writes=[ssq_t.buf])
                            p.op('act', lambda e, ssq_t=ssq_t, t1_t=t1_t: e.activation(
                                out=t1_t.ap, in_=ssq_t.ap, func=AF.Ln, bias=EPS, scale=1.0 / 64), reads=[ssq_t.buf], writes=[t1_t.buf])
                            p.op('act', lambda e, rs_t=rs_t, t1_t=t1_t: e.activation(
                                out=rs_t.ap, in_=t1_t.ap, func=AF.Exp, scale=-0.5), reads=[t1_t.buf], writes=[rs_t.buf])
                            p.op('dve', lambda e, b=b, tq_t=tq_t, rs_t=rs_t: e.tensor_tensor(
                                out=tq_t.ap.rearrange("p (a b) -> p a b", a=8),
                                in0=ctx.banks[b].rearrange("p (a b) -> p a b", a=8),
                                in1=rs_t.ap.rearrange("p (a b) -> p a b", b=1).broadcast_to([128, 8, 64]), op=ALU.mult),
                                reads=[ctx.bbuf[b], rs_t.buf], writes=[tq_t.buf])
                            p.op('dve', lambda e, tq_t=tq_t, t_t=t_t, nw=nw: e.tensor_tensor(
                                out=t_t.ap.rearrange("p (a b) -> p a b", a=8),
                                in0=tq_t.ap.rearrange("p (a b) -> p a b", a=8),
                                in1=nw.ap.rearrange("p (a b) -> p a b", a=1).broadcast_to([128, 8, 64]), op=ALU.mult),
                                reads=[tq_t.buf, nw.buf], writes=[t_t.buf])
                        r0 = st * 512 + u * 128
                        p.dma('sp', outs[g][r0:r0 + 128, :], t_t.ap, 'p2a_tst%d' % i, reads=[t_t.buf], writes_nw=[obuf[g]])
                        tcnt += 1
    return obuf


def build_B(dbg_stage=None):
    ctx = Ctx()
    cin = declare_consts(ctx)
    hT = ctx.dram_t("hT", [NST, 128, 16, 512], BF16, "ExternalInput")
    win = ctx.dram_t("win", [D, 4096], F32, "ExternalInput")
    qnw = ctx.dram_t("qnw", [1, 64], F32, "ExternalInput")
    knw = ctx.dram_t("knw", [1, 64], F32, "ExternalInput")
    IK = "ExternalOutput" if dbg_stage == 'p2a' else "Internal"
    outs = {}
    for g in range(8):
        if P2A_KIND[g] == 'F':
            outs[g] = ctx.dram_t("pj%d" % g, [4, 128, S], F32, IK)
        else:
            outs[g] = ctx.dram_t("pj%d" % g, [S, 512], BF16, IK)
    consts = load_consts(ctx, cin)
    mark = ctx.off
    hT_buf = Buf("hT_in")
    obuf = stage_p2a(ctx, consts, mark, hT, hT_buf, win, qnw, knw, outs)
    if dbg_stage == 'p2a':
        ctx.p.finish()
        return ctx.p.emit()
    cat = declare_attn_consts(ctx)
    lamp = ctx.dram_t("lamp", [1, 256], F32, "ExternalInput")
    plr = ctx.dram_t("plr", [1, 2], F32, "ExternalInput")
    slw = ctx.dram_t("slw", [128, 1], F32, "ExternalInput")
    mt = ctx.dram_t("mt", [2, 128, 8, TOKC], BF16, "ExternalOutput")
    mt_buf = Buf("mt_out")
    if dbg_stage != 'hgrn':
        stage_attn(ctx, consts, mark, outs[5], outs[6], outs[7], obuf, cat, lamp, plr, slw, mt, mt_buf)
    if dbg_stage == 'attn':
        ctx.p.finish()
        return ctx.p.emit()
    hc = declare_hgrn_consts(ctx)
    lbl = ctx.dram_t("lbl", [128, 32], F32, "ExternalInput")
    selr = ctx.dram_t("selr", [1, 4], F32, "ExternalInput")
    onw = ctx.dram_t("onw", [128, 1], F32, "ExternalInput")
    stage_hgrn(ctx, consts, mark, outs[0], outs[1], outs[2], outs[3], outs[4], obuf, hc, lbl, selr, onw, mt, mt_buf)
    ctx.p.finish()
    return ctx.p.emit()


def host_attn_consts(core_half):
    t = np.arange(S)
    tl = t % 512
    tl_hi = (tl // 128) * 128
    tl_lo = tl % 128
    sl = t % 128
    qaug = np.zeros((4, 6, S), np.float32)
    abias = np.zeros((128, 4, 32), np.float32)
    negm = np.zeros((128, 4), np.float32)
    for j in range(4):
        m = 2.0 ** (-(4 * core_half + j + 1))
        qaug[j] = np.stack([m * tl_hi, m * tl_lo, m * np.ones(S), 2 * m * tl_hi, 2 * m * tl_lo, 2 * m * np.ones(S)])
        abias[:, j, :] = -m * 128.0 * np.arange(32)[None, :]
        negm[:, j] = -m
    kaug = np.stack([-np.ones(S), -np.ones(S), sl, np.ones(S), np.ones(S), -sl]).astype(np.float32)
    pp = np.arange(128)[:, None]
    uu = np.arange(896)[None, :]
    dstrip = np.abs(uu - pp - 384).astype(np.float32)
    return {
        'c_qaug': qaug.astype(NPBF16), 'c_kaug': kaug.astype(NPBF16),
        'c_abias': abias.reshape(128, 128), 'c_negm': negm, 'c_dstrip': dstrip,
    }


def declare_attn_consts(ctx):
    return {
        'c_qaug': ctx.dram_t('c_qaug', [4, 6, S], BF16, "ExternalInput"),
        'c_kaug': ctx.dram_t('c_kaug', [6, S], BF16, "ExternalInput"),
        'c_abias': ctx.dram_t('c_abias', [128, 128], F32, "ExternalInput"),
        'c_negm': ctx.dram_t('c_negm', [128, 4], F32, "ExternalInput"),
        'c_dstrip': ctx.dram_t('c_dstrip', [128, 896], F32, "ExternalInput"),
    }


def stage_attn(ctx, consts, mark, tq_d, tk_d, tv_d, in_bufs, cat, lamp_row_d, pl_row_d, slw_d, mt_d, mt_buf):
    p = ctx.p
    ctx.stage_begin(mark)
    dstrip = ctx.f32("at_dstrip", 896)
    abias = ctx.f32("at_abias", 128)
    negm = ctx.f32("at_negm", 4)
    lamp = ctx.f32("at_lamp", 256)
    pl = ctx.f32("at_pl", 2)
    slw = ctx.f32("at_slw", 2)
    lprod = ctx.f32("at_lprod", 128)
    ls = ctx.f32("at_ls", 2)
    le = ctx.f32("at_le", 2)
    nl0 = ctx.f32("at_nl0", 2)
    neglam = ctx.f32("at_neglam", 2)
    slwe = ctx.f32("at_slwe", 2)
    p.dma('sp', dstrip.ap, cat['c_dstrip'], 'at_c0', writes=[dstrip.buf])
    p.dma('sp', abias.ap, cat['c_abias'], 'at_c1', writes=[abias.buf])
    p.dma('sp', negm.ap, cat['c_negm'], 'at_c2', writes=[negm.buf])
    p.dma('sp', lamp.ap, lamp_row_d.partition_broadcast(128), 'at_c3', writes=[lamp.buf])
    p.dma('sp', pl.ap, pl_row_d.partition_broadcast(128), 'at_c4', writes=[pl.buf])
    p.dma('sp', slw.ap[:, 0:1], slw_d, 'at_c5', writes=[slw.buf])
    lv = lamp.ap.rearrange("p (a b c) -> p a b c", a=2, b=2)
    p.op('dve', lambda e: e.tensor_tensor(out=lprod.ap.rearrange("p (a c) -> p a c", a=2), in0=lv[:, :, 0, :],
                                          in1=lv[:, :, 1, :], op=ALU.mult), reads=[lamp.buf], writes=[lprod.buf])
    p.op('dve', lambda e: e.tensor_reduce(out=ls.ap, in_=lprod.ap.rearrange("p (a c) -> p a c", a=2), axis=AX.X,
                                          op=ALU.add), reads=[lprod.buf], writes=[ls.buf])
    p.op('act', lambda e: e.activation(out=le.ap, in_=ls.ap, func=AF.Exp), reads=[ls.buf], writes=[le.buf])
    p.op('dve', lambda e: e.tensor_tensor(out=nl0.ap[:, 0:1], in0=le.ap[:, 1:2], in1=le.ap[:, 0:1], op=ALU.subtract),
         reads=[le.buf], writes=[nl0.buf])
    p.op('dve', lambda e: e.tensor_tensor(out=neglam.ap[:, 0:1], in0=nl0.ap[:, 0:1], in1=pl.ap[:, 0:1], op=ALU.subtract),
         reads=[nl0.buf, pl.buf], writes=[neglam.buf])
    p.op('dve', lambda e: e.tensor_tensor(out=slwe.ap[:, 0:1], in0=slw.ap[:, 0:1], in1=pl.ap[:, 1:2], op=ALU.mult),
         reads=[slw.buf, pl.buf], writes=[slwe.buf])

    QT = [ctx.bf16("at_QT", S), ctx.bf16("at_QT", S)]
    KT = [ctx.bf16("at_KT", S), ctx.bf16("at_KT", S)]
    qtok = ctx.bf16("at_qtok", 32, 128)
    ktok = ctx.bf16("at_ktok", 32, 128)
    V = ctx.bf16("at_V", 32, 128)
    pT = [ctx.bf16("at_pT", 512) for _ in range(4)]
    stmp = [ctx.f32("at_stmp", 512) for _ in range(2)]
    rz = [ctx.f32("at_rz", 512) for _ in range(2)]
    tt = [ctx.f32("at_tt", 512) for _ in range(2)]
    o_t = ctx.f32("at_o", 512)
    sq_t = ctx.f32("at_sq", 512)
    ln_t = ctx.f32("at_ln", 512)
    rstd = ctx.f32("at_rstd", 512)
    res = [ctx.bf16("at_res", 512) for _ in range(2)]
    ident, ones_bf, ones_f = consts['ident'], consts['ones_bf'], consts['ones_f']
    for m in range(2):
        p.dma('sp', KT[m].ap[64:70, :], cat['c_kaug'], 'at_KT%d' % m, writes=[KT[m].buf])
    tb = 0
    ecnt = 0
    rcnt = 0
    for j in range(4):
        p.dma('sp', qtok.ap, tq_d[:, j * 128:(j + 1) * 128].rearrange("(c p) e -> p c e", p=128), 'at_qtok',
              reads=[in_bufs[5]], writes=[qtok.buf])
        p.dma('sp', ktok.ap, tk_d[:, j * 128:(j + 1) * 128].rearrange("(c p) e -> p c e", p=128), 'at_ktok',
              reads=[in_bufs[6]], writes=[ktok.buf])
        p.dma('sp', V.ap, tv_d[:, j * 128:(j + 1) * 128].rearrange("(c p) e -> p c e", p=128), 'at_V',
              reads=[in_bufs[7]], writes=[V.buf])
        for m in range(2):
            p.dma('sp', QT[m].ap[64:70, :], cat['c_qaug'][j], 'at_QT%d' % m, writes=[QT[m].buf])
        for (src, dst) in ((qtok, QT), (ktok, KT)):
            for m in range(2):
                for grp in range(4):
                    b = tb % 4
                    tb += 1
                    bv = bank_bf(ctx, b)
                    for i in range(8):
                        p.op('pe', lambda e, bv=bv, i=i, src=src, grp=grp, m=m: e.transpose(
                            bv[0:64, i * 128:(i + 1) * 128], src.ap[:, grp * 8 + i, m * 64:(m + 1) * 64], ident.ap),
                            reads=[src.buf, ident.buf], writes=[ctx.bbuf[b]])
                    d_ap = dst[m].ap[0:64, grp * 1024:(grp + 1) * 1024]
                    if ecnt % 2 == 0:
                        p.op('act', lambda e, bv=bv, d_ap=d_ap: e.copy(out=d_ap, in_=bv[0:64, :]),
                             reads=[ctx.bbuf[b]], writes=[dst[m].buf])
                    else:
                        p.op('dve', lambda e, bv=bv, d_ap=d_ap: e.tensor_copy(out=d_ap, in_=bv[0:64, :]),
                             reads=[ctx.bbuf[b]], writes=[dst[m].buf])
                    ecnt += 1
        for qb in range(8):
            def qk(sc):
                for m in range(2):
                    slot = (2 * sc + m) % 4
                    b = slot
                    if sc * 128 + 128 <= qb * 512:
                        K, idx, diag = 67, (qb * 512 - sc * 128) // 128, False
                    elif sc * 128 >= qb * 512 + 512:
                        K, idx, diag = 70, (sc * 128 - qb * 512) // 128, False
                    else:
                        K, idx, diag = 64, 0, True
                    p.op('pe', lambda e, K=K, m=m, b=b, sc=sc, qb=qb: e.matmul(
                        ctx.banks[b], lhsT=KT[m].ap[0:K, sc * 128:(sc + 1) * 128], rhs=QT[m].ap[0:K, qb * 512:(qb + 1) * 512],
                        start=True, stop=True), reads=[KT[m].buf, QT[m].buf], writes=[ctx.bbuf[b]])
                    pt = pT[slot]
                    if diag:
                        jd = sc - 4 * qb
                        off = 384 - 128 * jd
                        st_t = stmp[m]
                        p.op('dve', lambda e, b=b, off=off, st_t=st_t, j=j: e.scalar_tensor_tensor(
                            out=st_t.ap, in0=dstrip.ap[:, off:off + 512], scalar=negm.ap[:, j:j + 1], in1=ctx.banks[b],
                            op0=ALU.mult, op1=ALU.add), reads=[dstrip.buf, negm.buf, ctx.bbuf[b]], writes=[st_t.buf])
                        p.op('act', lambda e, st_t=st_t, pt=pt: e.activation(out=pt.ap, in_=st_t.ap, func=AF.Exp),
                             reads=[st_t.buf], writes=[pt.buf])
                    else:
                        col = j * 32 + idx
                        p.op('act', lambda e, b=b, pt=pt, col=col: e.activation(
                            out=pt.ap, in_=ctx.banks[b], func=AF.Exp, bias=abias.ap[:, col:col + 1]),
                            reads=[ctx.bbuf[b], abias.buf], writes=[pt.buf])

            def pv(sc):
                for m in range(2):
                    pt = pT[(2 * sc + m) % 4]
                    p.op('pe', lambda e, m=m, sc=sc, pt=pt: e.matmul(
                        ctx.banks[4 + m], lhsT=V.ap[:, sc, :], rhs=pt.ap, start=(sc == 0), stop=(sc == 31)),
                        reads=[V.buf, pt.buf], writes=[ctx.bbuf[4 + m]])
                    p.op('pe', lambda e, m=m, sc=sc, pt=pt: e.matmul(
                        ctx.banks[6 + m], lhsT=ones_bf.ap, rhs=pt.ap, start=(sc == 0), stop=(sc == 31)),
                        reads=[ones_bf.buf, pt.buf], writes=[ctx.bbuf[6 + m]])

            qk(0)
            for sc in range(32):
                if sc + 1 < 32:
                    qk(sc + 1)
                pv(sc)
            for m in range(2):
                p.op('dve', lambda e, m=m: e.reciprocal(out=rz[m].ap, in_=ctx.banks[6 + m]),
                     reads=[ctx.bbuf[6 + m]], writes=[rz[m].buf])
                p.op('dve', lambda e, m=m: e.tensor_tensor(out=tt[m].ap, in0=ctx.banks[4 + m], in1=rz[m].ap, op=ALU.mult),
                     reads=[ctx.bbuf[4 + m], rz[m].buf], writes=[tt[m].buf])
            p.op('dve', lambda e: e.scalar_tensor_tensor(out=o_t.ap, in0=tt[1].ap, scalar=neglam.ap[:, 0:1], in1=tt[0].ap,
                                                         op0=ALU.mult, op1=ALU.add),
                 reads=[tt[0].buf, tt[1].buf, neglam.buf], writes=[o_t.buf])
            p.op('act', lambda e: e.activation(out=sq_t.ap, in_=o_t.ap, func=AF.Square), reads=[o_t.buf], writes=[sq_t.buf])
            p.op('pe', lambda e: e.matmul(ctx.banks[6], lhsT=ones_f.ap, rhs=sq_t.ap, start=True, stop=True),
                 reads=[ones_f.buf, sq_t.buf], writes=[ctx.bbuf[6]])
            p.op('act', lambda e: e.activation(out=ln_t.ap, in_=ctx.banks[6], func=AF.Ln, bias=EPS, scale=1.0 / 128),
                 reads=[ctx.bbuf[6]], writes=[ln_t.buf])
            p.op('act', lambda e: e.activation(out=rstd.ap, in_=ln_t.ap, func=AF.Exp, scale=-0.5),
                 reads=[ln_t.buf], writes=[rstd.buf])
            r_t = res[rcnt % 2]
            p.op('dve', lambda e, r_t=r_t: e.scalar_tensor_tensor(out=r_t.ap, in0=o_t.ap, scalar=slwe.ap[:, 0:1], in1=rstd.ap,
                                                                  op0=ALU.mult, op1=ALU.mult),
                 reads=[o_t.buf, slwe.buf, rstd.buf], writes=[r_t.buf])
            half, off = qb // 4, (qb % 4) * 512
            p.dma('sp', mt_d[half, :, 4 + j, off:off + 512], r_t.ap, 'at_res%d' % (rcnt % 2), reads=[r_t.buf],
                  writes_nw=[mt_buf])
            rcnt += 1


LB_CEIL = 1.0 - 1e-6


def host_hgrn_consts():
    seg = np.ones((1, S), np.float32)
    seg[0, ::64] = 0.0
    s_ = np.arange(64)[:, None]
    t_ = np.arange(64)[None, :]
    tri = np.concatenate([(s_ <= t_), (s_ >= t_)], axis=1).astype(np.float32)
    return {'c_seg': seg.astype(NPBF16), 'c_tri': tri}


def declare_hgrn_consts(ctx):
    return {'c_seg': ctx.dram_t('c_seg', [1, S], BF16, "ExternalInput"),
            'c_tri': ctx.dram_t('c_tri', [64, 128], F32, "ExternalInput")}


class BT:
    def __init__(self, t):
        self.ap = t.ap
        self.bufs = [Buf(t.buf.name + "_b%d" % i) for i in range(4)]

    def blk(self, i):
        return self.ap[:, i * 1024:(i + 1) * 1024]

    def blk3(self, i):
        return self.ap[:, i * 1024:(i + 1) * 1024].rearrange("p (a b) -> p a b", b=64)


def stage_hgrn(ctx, consts, mark, fq_d, fzf_d, fzb_d, tvh_d, fg_d, in_bufs, hc, lbl_d, sel_row_d, onw_d, mt_d, mt_buf):
    p = ctx.p
    ctx.stage_begin(mark)
    ident, ones_f = consts['ident'], consts['ones_f']
    lbl = ctx.f32("hg_lbl", 32)
    sel = ctx.f32("hg_sel", 4)
    onw = ctx.f32("hg_onw", 2)
    le = ctx.f32("hg_le", 32)
    lsel = ctx.f32("hg_lsel", 32)
    tot = ctx.f32("hg_tot", 8)
    num = ctx.f32("hg_num", 8)
    lb = ctx.f32("hg_lb", 8)
    oml = ctx.f32("hg_oml", 8)
    tri = ctx.f32("hg_tri", 128, parts=64)
    seg = ctx.bf16("hg_seg", S)
    p.dma('sp', lbl.ap, lbl_d, 'hg_c0', writes=[lbl.buf])
    p.dma('sp', sel.ap, sel_row_d.partition_broadcast(128), 'hg_c1', writes=[sel.buf])
    p.dma('sp', onw.ap[:, 0:1], onw_d, 'hg_c2', writes=[onw.buf])
    p.dma('sp', tri.ap, hc['c_tri'], 'hg_c3', writes=[tri.buf])
    p.dma('sp', seg.ap, hc['c_seg'].partition_broadcast(128), 'hg_c4', writes=[seg.buf])
    p.op('act', lambda e: e.activation(out=le.ap, in_=lbl.ap, func=AF.Exp), reads=[lbl.buf], writes=[le.buf])
    p.op('dve', lambda e: e.tensor_reduce(out=tot.ap, in_=le.ap.rearrange("p (a b) -> p a b", b=4), axis=AX.X, op=ALU.add),
         reads=[le.buf], writes=[tot.buf])
    p.op('dve', lambda e: e.tensor_tensor(out=lsel.ap.rearrange("p (a b) -> p a b", b=4),
                                          in0=le.ap.rearrange("p (a b) -> p a b", b=4),
                                          in1=sel.ap.rearrange("p (a b) -> p a b", a=1).broadcast_to([128, 8, 4]), op=ALU.mult),
         reads=[le.buf, sel.buf], writes=[lsel.buf])
    p.op('dve', lambda e: e.tensor_reduce(out=num.ap, in_=lsel.ap.rearrange("p (a b) -> p a b", b=4), axis=AX.X, op=ALU.add),
         reads=[lsel.buf], writes=[num.buf])
    p.op('dve', lambda e: e.reciprocal(out=tot.ap, in_=tot.ap), reads=[tot.buf], writes=[tot.buf])
    p.op('dve', lambda e: e.tensor_tensor(out=lb.ap, in0=num.ap, in1=tot.ap, op=ALU.mult), reads=[num.buf, tot.buf], writes=[lb.buf])
    p.op('dve', lambda e: e.tensor_scalar(out=lb.ap, in0=lb.ap, scalar1=LB_CEIL, scalar2=None, op0=ALU.min),
         reads=[lb.buf], writes=[lb.buf])
    p.op('dve', lambda e: e.tensor_scalar(out=oml.ap, in0=lb.ap, scalar1=-1.0, scalar2=1.0, op0=ALU.mult, op1=ALU.add),
         reads=[lb.buf], writes=[oml.buf])

    Qt = BT(ctx.f32("hg_Q", S))
    Zt = BT(ctx.f32("hg_Z", S))
    At = BT(ctx.f32("hg_A", S))
    Kt = BT(ctx.f32("hg_K", S))
    Ot = BT(ctx.f32("hg_O", S))
    qa = BT(ctx.bf16("hg_qa", S))
    qb = BT(ctx.bf16("hg_qb", S))
    kb = BT(ctx.bf16("hg_kb", S))
    kh = BT(ctx.bf16("hg_kh", S))
    V = ctx.bf16("hg_V", 64, 128, parts=64)
    khT = ctx.bf16("hg_khT", 64, 128, parts=64)
    St = ctx.f32("hg_S", 128)
    Sbf = ctx.bf16("hg_Sbf", 128)
    scm = [ctx.bf16("hg_scm", 64, parts=64), ctx.bf16("hg_scm", 64, parts=64)]
    ealast = ctx.f32("hg_ealast", 64)
    eamid = ctx.f32("hg_eamid", 64)
    dlt = ctx.f32("hg_dlt", 64)
    eald = ctx.f32("hg_eald", 64)
    sq_t = ctx.f32("hg_sq", 512)
    ln_t = ctx.f32("hg_ln", 512)
    rstd = ctx.f32("hg_rstd", 512)
    t_t = ctx.f32("hg_t", 512)
    sg_t = ctx.f32("hg_sg", 512)
    res = [ctx.bf16("hg_res", 512), ctx.bf16("hg_res", 512)]
    rcnt = 0
    tcnt = 0
    for j in range(4):
        for bk in range(4):
            p.dma('sp', Qt.blk(bk), fq_d[j, :, bk * 1024:(bk + 1) * 1024], 'hg_Q%d' % bk, reads=[in_bufs[0]], writes=[Qt.bufs[bk]])
        p.dma('sp', V.ap, tvh_d[:, j * 128:(j + 1) * 128].rearrange("(c p) e -> p c e", p=64), 'hg_V',
              reads=[in_bufs[3]], writes=[V.buf])
        for dr in range(2):
            fz_d = fzf_d if dr == 0 else fzb_d
            col = dr * 4 + j
            last_i, mid_i = (63, 31) if dr == 0 else (0, 32)
            for bk in range(4):
                zb, ab, kb_, qb_ = Zt.bufs[bk], At.bufs[bk], Kt.bufs[bk], Qt.bufs[bk]
                Z, A, K_, Q = Zt.blk(bk), At.blk(bk), Kt.blk(bk), Qt.blk(bk)
                Z3, A3 = Zt.blk3(bk), At.blk3(bk)
                p.dma('sp', Z, fz_d[j, :, bk * 1024:(bk + 1) * 1024], 'hg_Z%d' % bk, reads=[in_bufs[1 + dr]], writes=[zb])
                p.op('act', lambda e, Z=Z: e.activation(out=Z, in_=Z, func=AF.Sigmoid), reads=[zb], writes=[zb])
                p.op('dve', lambda e, Z=Z, col=col: e.tensor_scalar(out=Z, in0=Z, scalar1=oml.ap[:, col:col + 1],
                                                                   scalar2=lb.ap[:, col:col + 1], op0=ALU.mult, op1=ALU.add),
                     reads=[zb, oml.buf, lb.buf], writes=[zb])
                p.op('pool', lambda e, Z=Z, K_=K_: e.tensor_scalar(out=K_, in0=Z, scalar1=-1.0, scalar2=1.0, op0=ALU.mult,
                                                                  op1=ALU.add), reads=[zb], writes=[kb_])
                p.op('act', lambda e, Z=Z: e.activation(out=Z, in_=Z, func=AF.Ln), reads=[zb], writes=[zb])
                sg_ap = seg.ap[:, bk * 1024:(bk + 1) * 1024]
                p.op('dve', lambda e, Z=Z, A=A, sg_ap=sg_ap: e.tensor_tensor_scan(out=A, data0=sg_ap, data1=Z, initial=0.0,
                                                                               op0=ALU.mult, op1=ALU.add),
                     reads=[zb, seg.buf], writes=[ab])
                if dr == 1:
                    p.op('dve', lambda e, Z3=Z3, A3=A3: e.tensor_tensor(out=Z3, in0=Z3, in1=A3[:, :, 63:64].broadcast_to([128, 16, 64]),
                                                                      op=ALU.add), reads=[zb, ab], writes=[zb])
                    p.op('dve', lambda e, Z=Z, A=A: e.tensor_tensor(out=A, in0=Z, in1=A, op=ALU.subtract), reads=[zb, ab], writes=[ab])
                cs = slice(bk * 16, (bk + 1) * 16)
                a_last = A3[:, :, last_i]
                a_mid = A3[:, :, mid_i]
                p.op('act', lambda e, a_last=a_last, cs=cs: e.activation(out=ealast.ap[:, cs], in_=a_last, func=AF.Exp),
                     reads=[ab], writes=[ealast.buf])
                p.op('act', lambda e, a_mid=a_mid, cs=cs: e.activation(out=eamid.ap[:, cs], in_=a_mid, func=AF.Exp),
                     reads=[ab], writes=[eamid.buf])
                p.op('dve', lambda e, a_last=a_last, a_mid=a_mid, cs=cs: e.tensor_tensor(out=dlt.ap[:, cs], in0=a_last, in1=a_mid,
                                                                                    op=ALU.subtract), reads=[ab], writes=[dlt.buf])
                p.op('act', lambda e, cs=cs: e.activation(out=eald.ap[:, cs], in_=dlt.ap[:, cs], func=AF.Exp),
                     reads=[dlt.buf], writes=[eald.buf])
                p.op('dve', lambda e, Z3=Z3, A3=A3, mid_i=mid_i: e.tensor_tensor(
                    out=Z3, in0=A3, in1=A3[:, :, mid_i:mid_i + 1].broadcast_to([128, 16, 64]), op=ALU.subtract),
                    reads=[ab], writes=[zb])
                p.op('act', lambda e, Z=Z, A=A: e.activation(out=A, in_=Z, func=AF.Exp, scale=-1.0), reads=[zb], writes=[ab])
                p.op('act', lambda e, Z=Z: e.activation(out=Z, in_=Z, func=AF.Exp), reads=[zb], writes=[zb])
                p.op('dve', lambda e, Z=Z, Q=Q: e.tensor_tensor(out=Z, in0=Q, in1=Z, op=ALU.mult), reads=[zb, qb_], writes=[zb])
                p.op('pool', lambda e, A=A, K_=K_: e.tensor_tensor(out=A, in0=K_, in1=A, op=ALU.mult), reads=[ab, kb_], writes=[ab])
                p.op('act', lambda e, Z=Z, bk=bk: e.copy(out=qb.blk(bk), in_=Z), reads=[zb], writes=[qb.bufs[bk]])
                p.op('pool', lambda e, A=A, bk=bk: e.tensor_copy(out=kb.blk(bk), in_=A), reads=[ab], writes=[kb.bufs[bk]])
                p.op('dve', lambda e, Z3=Z3, bk=bk, cs=cs: e.tensor_tensor(
                    out=qa.blk3(bk), in0=Z3, in1=eamid.ap[:, cs].rearrange("p (a b) -> p a b", b=1).broadcast_to([128, 16, 64]),
                    op=ALU.mult), reads=[zb, eamid.buf], writes=[qa.bufs[bk]])
                p.op('pool', lambda e, A3=A3, bk=bk, cs=cs: e.tensor_tensor(
                    out=kh.blk3(bk), in0=A3, in1=eald.ap[:, cs].rearrange("p (a b) -> p a b", b=1).broadcast_to([128, 16, 64]),
                    op=ALU.mult), reads=[ab, eald.buf], writes=[kh.bufs[bk]])
            for grp in range(8):
                b = 6 + (tcnt % 2)
                tcnt += 1
                bv = bank_bf(ctx, b)
                for i in range(8):
                    ch = grp * 8 + i
                    p.op('pe', lambda e, bv=bv, i=i, ch=ch: e.transpose(bv[0:64, i * 128:(i + 1) * 128],
                                                                       kh.ap[:, ch * 64:(ch + 1) * 64], ident.ap),
                         reads=[kh.bufs[ch // 16], ident.buf], writes=[ctx.bbuf[b]])
                d_ap = khT.ap[:, grp * 8:(grp + 1) * 8, :]
                s_ap = bv[0:64, :].rearrange("p (a b) -> p a b", a=8)
                if grp % 2 == 0:
                    p.op('act', lambda e, d_ap=d_ap, s_ap=s_ap: e.copy(out=d_ap, in_=s_ap), reads=[ctx.bbuf[b]], writes=[khT.buf])
                else:
                    p.op('dve', lambda e, d_ap=d_ap, s_ap=s_ap: e.tensor_copy(out=d_ap, in_=s_ap), reads=[ctx.bbuf[b]], writes=[khT.buf])
            order = list(range(NCH)) if dr == 0 else list(range(NCH - 1, -1, -1))
            tri_ap = tri.ap[:, dr * 64:(dr + 1) * 64]
            for idx, ch in enumerate(order):
                first = (idx == 0)
                lastc = (idx == NCH - 1)
                bs = idx % 2
                bo = 2 + ((idx // 8) % 2)
                bd = 4 + (idx % 2)
                blk = ch // 16
                cc = slice(ch * 64, (ch + 1) * 64)
                sc_t = scm[idx % 2]
                p.op('pe', lambda e, bs=bs, cc=cc: e.matmul(ctx.banks[bs][0:64, 0:64], lhsT=kb.ap[:, cc], rhs=qb.ap[:, cc],
                                                            start=True, stop=True),
                     reads=[kb.bufs[blk], qb.bufs[blk]], writes=[ctx.bbuf[bs]])
                p.op('dve', lambda e, bs=bs, sc_t=sc_t, tri_ap=tri_ap: e.tensor_tensor(out=sc_t.ap, in0=ctx.banks[bs][0:64, 0:64],
                                                                                   in1=tri_ap, op=ALU.mult),
                     reads=[ctx.bbuf[bs], tri.buf], writes=[sc_t.buf])
                oc = slice((ch % 8) * 64, (ch % 8 + 1) * 64)
                p.op('pe', lambda e, bo=bo, oc=oc, ch=ch, sc_t=sc_t, first=first: e.matmul(
                    ctx.banks[bo][:, oc], lhsT=V.ap[:, ch, :], rhs=sc_t.ap, start=True, stop=first),
                    reads=[V.buf, sc_t.buf], writes=[ctx.bbuf[bo]])
                if not first:
                    p.op('pe', lambda e, bo=bo, oc=oc, cc=cc: e.matmul(ctx.banks[bo][:, oc], lhsT=Sbf.ap, rhs=qa.ap[:, cc],
                                                                       start=False, stop=True),
                         reads=[Sbf.buf, qa.bufs[blk]], writes=[ctx.bbuf[bo]])
                if not lastc:
                    p.op('pe', lambda e, bd=bd, ch=ch: e.matmul(ctx.banks[bd][:, 0:128], lhsT=khT.ap[:, ch, :], rhs=V.ap[:, ch, :],
                                                                start=True, stop=True),
                         reads=[khT.buf, V.buf], writes=[ctx.bbuf[bd]])
                    if first:
                        p.op('dve', lambda e, bd=bd: e.tensor_copy(out=St.ap, in_=ctx.banks[bd][:, 0:128]),
                             reads=[ctx.bbuf[bd]], writes=[St.buf])
                    else:
                        p.op('dve', lambda e, bd=bd, ch=ch: e.scalar_tensor_tensor(
                            out=St.ap, in0=St.ap, scalar=ealast.ap[:, ch:ch + 1], in1=ctx.banks[bd][:, 0:128],
                            op0=ALU.mult, op1=ALU.add), reads=[St.buf, ealast.buf, ctx.bbuf[bd]], writes=[St.buf])
                    p.op('act', lambda e: e.copy(out=Sbf.ap, in_=St.ap), reads=[St.buf], writes=[Sbf.buf])
                if idx % 8 == 7:
                    g8 = ch // 8
                    o_ap = Ot.ap[:, g8 * 512:(g8 + 1) * 512]
                    ob = Ot.bufs[g8 // 2]
                    if dr == 0:
                        p.op('act', lambda e, bo=bo, o_ap=o_ap: e.copy(out=o_ap, in_=ctx.banks[bo]),
                             reads=[ctx.bbuf[bo]], writes=[ob])
                    else:
                        p.op('dve', lambda e, bo=bo, o_ap=o_ap: e.tensor_tensor(out=o_ap, in0=ctx.banks[bo], in1=o_ap, op=ALU.add),
                             reads=[ctx.bbuf[bo], ob], writes=[ob])
        for bk in range(4):
            p.dma('sp', Zt.blk(bk), fg_d[j, :, bk * 1024:(bk + 1) * 1024], 'hg_Z%d' % bk, reads=[in_bufs[4]], writes=[Zt.bufs[bk]])
        for b8 in range(8):
            o_ap = Ot.ap[:, b8 * 512:(b8 + 1) * 512]
            g_ap = Zt.ap[:, b8 * 512:(b8 + 1) * 512]
            ob, gb = Ot.bufs[b8 // 2], Zt.bufs[b8 // 2]
            bn = 6 + (b8 % 2)
            p.op('act', lambda e, o_ap=o_ap: e.activation(out=sq_t.ap, in_=o_ap, func=AF.Square), reads=[ob], writes=[sq_t.buf])
            p.op('pe', lambda e, bn=bn: e.matmul(ctx.banks[bn], lhsT=ones_f.ap, rhs=sq_t.ap, start=True, stop=True),
                 reads=[ones_f.buf, sq_t.buf], writes=[ctx.bbuf[bn]])
            p.op('act', lambda e, bn=bn: e.activation(out=ln_t.ap, in_=ctx.banks[bn], func=AF.Ln, bias=EPS, scale=1.0 / 128),
                 reads=[ctx.bbuf[bn]], writes=[ln_t.buf])
            p.op('act', lambda e: e.activation(out=rstd.ap, in_=ln_t.ap, func=AF.Exp, scale=-0.5), reads=[ln_t.buf], writes=[rstd.buf])
            p.op('dve', lambda e, o_ap=o_ap: e.scalar_tensor_tensor(out=t_t.ap, in0=o_ap, scalar=onw.ap[:, 0:1], in1=rstd.ap,
                                                                    op0=ALU.mult, op1=ALU.mult),
                 reads=[ob, onw.buf, rstd.buf], writes=[t_t.buf])
            p.op('act', lambda e, g_ap=g_ap: e.activation(out=sg_t.ap, in_=g_ap, func=AF.Silu), reads=[gb], writes=[sg_t.buf])
            r_t = res[rcnt % 2]
            p.op('dve', lambda e, r_t=r_t: e.tensor_tensor(out=r_t.ap, in0=t_t.ap, in1=sg_t.ap, op=ALU.mult),
                 reads=[t_t.buf, sg_t.buf], writes=[r_t.buf])
            half, off = b8 // 4, (b8 % 4) * 512
            p.dma('sp', mt_d[half, :, j, off:off + 512], r_t.ap, 'hg_res%d' % (rcnt % 2), reads=[r_t.buf], writes_nw=[mt_buf])
            rcnt += 1


_PROGS = {}


def _prog(name):
    if name not in _PROGS:
        if name == 'A':
            _PROGS[name] = build_A
        elif name == 'B':
            _PROGS[name] = lambda: build_B(None)
        elif name == 'C':
            _PROGS[name] = lambda: build_C(True)
    return _PROGS[name]()


def lam_init_of(l):
    return 0.8 - 0.6 * math.exp(-0.3 * l)


def wout_perm():
    idx = []
    for r in range(2):
        for c in range(8):
            if c < 4:
                base = (4 * r + c) * 128
            else:
                base = 1024 + (4 * r + c - 4) * 128
            idx.extend(range(base, base + 128))
    return np.asarray(idx)


def win_cols(i):
    idx = []
    for g in range(8):
        idx.extend(range(g * 1024 + i * 512, g * 1024 + (i + 1) * 512))
    return np.asarray(idx)


def kernel(x, norm_mix_w, w_in, hg_lb_logits, hg_onorm_w, da_qnorm_w, da_knorm_w, da_lambda, da_subln_w,
           w_out, norm_ffn_w, w_gate, w_up, w_down):
    f32 = np.float32
    x = np.asarray(x, f32)
    norm_mix_w = np.asarray(norm_mix_w, f32)
    w_in = np.asarray(w_in, f32)
    hg_lb_logits = np.asarray(hg_lb_logits, f32)
    hg_onorm_w = np.asarray(hg_onorm_w, f32)
    da_qnorm_w = np.asarray(da_qnorm_w, f32)
    da_knorm_w = np.asarray(da_knorm_w, f32)
    da_lambda = np.asarray(da_lambda, f32)
    da_subln_w = np.asarray(da_subln_w, f32)
    w_out = np.asarray(w_out, f32)
    norm_ffn_w = np.asarray(norm_ffn_w, f32)
    w_gate = np.asarray(w_gate, f32)
    w_up = np.asarray(w_up, f32)
    w_down = np.asarray(w_down, f32)
    cores = list(range(8))
    hc = host_consts()
    ac = [host_attn_consts(0), host_attn_consts(1)]
    gc = host_hgrn_consts()
    perm = wout_perm()
    lbl_c = []
    for i in range(2):
        t = hg_lb_logits[:, :, i * 512:(i + 1) * 512].reshape(2, DEPTH, 4, 128)
        lbl_c.append(np.ascontiguousarray(t.transpose(3, 0, 2, 1)).reshape(128, 32))
    x_own = [np.ascontiguousarray(x[c // 2, (c % 2) * TOKC:(c % 2 + 1) * TOKC]) for c in cores]
    res = run_bass_kernel_spmd(_prog('A'), [dict(hc, x=x_own[c], nmw=norm_mix_w[0:1]) for c in cores], core_ids=cores)
    hT_own = [np.asarray(res.results[c]["hT"]) for c in cores]
    for l in range(DEPTH):
        li = lam_init_of(l)
        wins = [np.ascontiguousarray(w_in[l][:, win_cols(i)]) for i in range(2)]
        sel = np.array([[0.0] + [1.0 if ll <= l else 0.0 for ll in range(1, DEPTH)]], f32)
        in_b = []
        for c in cores:
            b, i = c // 2, c % 2
            hT_full = np.concatenate([hT_own[2 * b], hT_own[2 * b + 1]], axis=0)
            d = dict(hc, hT=hT_full, win=wins[i], qnw=da_qnorm_w[l:l + 1], knw=da_knorm_w[l:l + 1],
                     lamp=da_lambda[l].reshape(1, 256), plr=np.array([[li, 1.0 - li]], f32),
                     slw=np.ascontiguousarray(da_subln_w[l].reshape(128, 1)), lbl=lbl_c[i], selr=sel,
                     onw=np.ascontiguousarray(hg_onorm_w[l].reshape(128, 1)))
            d.update(ac[i])
            d.update(gc)
            in_b.append(d)
        res = run_bass_kernel_spmd(_prog('B'), in_b, core_ids=cores)
        mt_send = [np.asarray(res.results[c]["mt"]) for c in cores]
        wout_l = np.ascontiguousarray(w_out[l][perm])
        in_c = []
        for c in cores:
            b, i = c // 2, c % 2
            mt_recv = np.stack([mt_send[2 * b][i], mt_send[2 * b + 1][i]], axis=0)
            in_c.append(dict(hc, x=x_own[c], mt=mt_recv, wout=wout_l, nfw=norm_ffn_w[l:l + 1], wg=w_gate[l], wu=w_up[l],
                             wd=w_down[l], nmw=norm_mix_w[min(l + 1, DEPTH - 1):min(l + 1, DEPTH - 1) + 1]))
        res = run_bass_kernel_spmd(_prog('C'), in_c, core_ids=cores)
        x_own = [np.asarray(res.results[c]["x2"]) for c in cores]
        hT_own = [np.asarray(res.results[c]["hT"]) for c in cores]
    out = np.empty((NBATCH, S, D), f32)
    for c in cores:
        out[c // 2, (c % 2) * TOKC:(c % 2 + 1) * TOKC] = x_own[c]
    return out
```
